# Optimizing a Trainium2 kernel written in Bass

```python
import jax, jax.numpy as jnp
from jax import lax
import numpy as np

D_MODEL = 1024
BATCH = 8
SEQ = 2048
DEPTH = 4

HEAD_DIM = 64
CONV_HEADS = 4
ATTN_HEADS = 8
SGU_HEADS = 4
CONV_W = CONV_HEADS * HEAD_DIM
ATTN_W = ATTN_HEADS * HEAD_DIM
SGU_W = SGU_HEADS * HEAD_DIM
D_MIX = CONV_W + ATTN_W + SGU_W
D_IN_PROJ = 3 * CONV_W + 3 * ATTN_W + 2 * SGU_W
CONV_WIDTH = 3
Q_BLOCK = 128
CHUNK = 128
D_FF = 4 * D_MODEL
PLE_DIM = 256
EPS = 1e-6

kernel_name = "hybrid_conv_stickbreak_sgu_block"


def rms_norm(x, g):
    xf = x.astype(jnp.float32)
    y = xf * lax.rsqrt(jnp.mean(xf * xf, axis=-1, keepdims=True) + EPS)
    return (y * g.astype(jnp.float32)).astype(x.dtype)


def short_conv(x, w):
    s = x.shape[1]
    xp = jnp.pad(x, ((0, 0), (CONV_WIDTH - 1, 0), (0, 0)))
    out = w[0] * xp[:, 0:s]
    for j in range(1, CONV_WIDTH):
        out = out + w[j] * xp[:, j:j + s]
    return out


def stick_breaking_attention(q, k, v):
    s_len = q.shape[1]
    scale = HEAD_DIM ** -0.5
    outs = []
    for qb in range(s_len // Q_BLOCK):
        start = qb * Q_BLOCK
        end = start + Q_BLOCK
        qi = q[:, start:end].astype(jnp.float32)
        kj = k[:, :end].astype(jnp.float32)
        vj = v[:, :end].astype(jnp.float32)
        z = jnp.einsum('bqhd,bkhd->bhqk', qi, kj) * scale
        t_pos = start + jnp.arange(Q_BLOCK)[:, None]
        s_pos = jnp.arange(end)[None, :]
        causal = s_pos < t_pos
        log_beta = jax.nn.log_sigmoid(z)
        log_rem = jnp.where(causal, jax.nn.log_sigmoid(-z), 0.0)
        suffix = lax.cumsum(log_rem, axis=3, reverse=True) - log_rem
        weights = jnp.where(causal, jnp.exp(log_beta + suffix), 0.0)
        o = jnp.einsum('bhqk,bkhd->bqhd', weights, vj)
        outs.append(o.astype(v.dtype))
    return jnp.concatenate(outs, axis=1)


def spatial_gating(u, v, g_v, w_s, b_s):
    bsz, s_len, _ = u.shape
    v = v.reshape(bsz, s_len, SGU_HEADS, HEAD_DIM)
    v = rms_norm(v, g_v.reshape(SGU_HEADS, HEAD_DIM))
    v = v.reshape(bsz, s_len // CHUNK, CHUNK, SGU_HEADS, HEAD_DIM)
    mask = jnp.tril(jnp.ones((CHUNK, CHUNK), dtype=w_s.dtype))
    w = w_s * mask
    sv = jnp.einsum('gts,bcsge->bctge', w, v) + b_s.T[:, :, None]
    return u * sv.reshape(bsz, s_len, SGU_W)


def setup_inputs(seed: int = 0) -> dict:
    key = jax.random.key(seed)
    ks = jax.random.split(key, 17)

    def nrm(k, shape, scale):
        return jax.random.normal(k, shape, jnp.float32) * scale

    def gain(k, shape):
        return 1.0 + 0.05 * jax.random.normal(k, shape, jnp.float32)

    return {
        "x": nrm(ks[0], (BATCH, SEQ, D_MODEL), 1.0),
        "p": nrm(ks[1], (DEPTH, BATCH, SEQ, PLE_DIM), 1.0),
        "norm1_g": gain(ks[2], (DEPTH, D_MODEL)),
        "w_in": nrm(ks[3], (DEPTH, D_MODEL, D_IN_PROJ), D_MODEL ** -0.5),
        "conv_w": nrm(ks[4], (DEPTH, CONV_WIDTH, CONV_W), CONV_WIDTH ** -0.5),
        "q_norm_g": gain(ks[5], (DEPTH, HEAD_DIM)),
        "k_norm_g": gain(ks[6], (DEPTH, HEAD_DIM)),
        "sgu_norm_g": gain(ks[7], (DEPTH, SGU_W)),
        "sgu_w": nrm(ks[8], (DEPTH, SGU_HEADS, CHUNK, CHUNK), CHUNK ** -0.5),
        "sgu_b": gain(ks[9], (DEPTH, SGU_HEADS, CHUNK)),
        "w_out": nrm(ks[10], (DEPTH, D_MIX, D_MODEL), D_MIX ** -0.5),
        "norm2_g": gain(ks[11], (DEPTH, D_MODEL)),
        "w_ff1": nrm(ks[12], (DEPTH, D_MODEL, D_FF), D_MODEL ** -0.5),
        "w_ff2": nrm(ks[13], (DEPTH, D_FF, D_MODEL), D_FF ** -0.5),
        "norm3_g": gain(ks[14], (DEPTH, D_MODEL)),
        "w_ple_gate": nrm(ks[15], (DEPTH, D_MODEL, D_MODEL), D_MODEL ** -0.5),
        "w_ple_proj": nrm(ks[16], (DEPTH, PLE_DIM, D_MODEL), PLE_DIM ** -0.5),
    }


def reference(x, p, norm1_g, w_in, conv_w, q_norm_g, k_norm_g, sgu_norm_g, sgu_w, sgu_b,
              w_out, norm2_g, w_ff1, w_ff2, norm3_g, w_ple_gate, w_ple_proj):
    bsz, s_len, _ = x.shape
    split_idx = list(np.cumsum([CONV_W, CONV_W, CONV_W, ATTN_W, ATTN_W, ATTN_W, SGU_W]))
    h = x
    for i in range(DEPTH):
        hn = rms_norm(h, norm1_g[i])
        proj = hn @ w_in[i]
        a_b, a_c, a_h, q, k, v, c_u, c_v = jnp.split(proj, split_idx, axis=-1)
        y_a = a_b * short_conv(a_c * a_h, conv_w[i])
        q = rms_norm(q.reshape(bsz, s_len, ATTN_HEADS, HEAD_DIM), q_norm_g[i])
        k = rms_norm(k.reshape(bsz, s_len, ATTN_HEADS, HEAD_DIM), k_norm_g[i])
        v = v.reshape(bsz, s_len, ATTN_HEADS, HEAD_DIM)
        y_b = stick_breaking_attention(q, k, v).reshape(bsz, s_len, ATTN_W)
        y_c = spatial_gating(jax.nn.gelu(c_u, approximate=False),
                             jax.nn.gelu(c_v, approximate=False),
                             sgu_norm_g[i], sgu_w[i], sgu_b[i])
        h = h + jnp.concatenate([y_a, y_b, y_c], axis=-1) @ w_out[i]
        f = jnp.square(jax.nn.relu(rms_norm(h, norm2_g[i]) @ w_ff1[i]))
        h = h + f @ w_ff2[i]
        gate = jax.nn.sigmoid(rms_norm(h, norm3_g[i]) @ w_ple_gate[i])
        h = h + gate * (p[i] @ w_ple_proj[i])
    return h
```

```python
import numpy as np
import concourse.bass as bass
import concourse.mybir as mybir
from concourse.bass_utils import run_bass_kernel_spmd

F32 = mybir.dt.float32
BF16 = mybir.dt.bfloat16
AF = mybir.ActivationFunctionType
ALU = mybir.AluOpType
AX = mybir.AxisListType

DSZ = {F32: 4, BF16: 2}
BLKB = 256


def _dsize(dt):
    for k, v in DSZ.items():
        if dt == k:
            return v
    raise ValueError(str(dt))


class Op:
    __slots__ = ("eng", "fn", "deps", "is_dma", "sem", "sigval", "signal", "idx", "raw_same")

    def __init__(self, eng, fn, is_dma):
        self.eng = eng
        self.fn = fn
        self.is_dma = is_dma
        self.deps = set()
        self.sem = None
        self.sigval = None
        self.signal = False


class Prog:
    ENGS = ("pe", "act", "dve", "pool", "sp")

    def __init__(self, nc, n_dma_sems=8):
        self.nc = nc
        self.ops = []
        self.res = {}
        self.n_dma_sems = n_dma_sems
        self.dma_rr = {"sp": 0, "pool": 0, "act": 0}
        self.dma_last = {}
        self.final_ops = []

    def _keys(self, ap):
        t = ap.tensor
        tn = type(t).__name__
        if tn.startswith("DRam"):
            return ()
        sz = _dsize(ap.dtype)
        pat = ap.ap
        pstride, npart = pat[0]
        off = ap.offset
        if pstride == 0:
            pstride = 1 << 30
        p0 = off // pstride
        f0 = (off % pstride) * sz
        ext = 0
        for st, cnt in pat[1:]:
            ext += (cnt - 1) * abs(st)
        f1 = f0 + ext * sz + sz - 1
        name = t.name
        if tn.startswith("PSum"):
            return [(name, pg, -1) for pg in range(p0 // 64, (p0 + npart - 1) // 64 + 1)]
        keys = []
        for pg in range(p0 // 64, (p0 + npart - 1) // 64 + 1):
            for b in range(f0 // BLKB, f1 // BLKB + 1):
                keys.append((name, pg, b))
        return keys

    def add(self, eng, fn, reads=(), writes=(), dma=False):
        op = Op(eng, fn, dma)
        op.idx = len(self.ops)
        rk = []
        for ap in reads:
            rk.extend(self._keys(ap))
        wk = []
        for ap in writes:
            wk.extend(self._keys(ap))
        wk.extend(k for k in rk if k[2] == -1)
        rk = [k for k in rk if k[2] != -1]
        res = self.res
        for k in rk:
            e = res.get(k)
            if e is not None and e[0] is not None:
                self._dep(op, e[0], raw=True)
        for k in wk:
            e = res.get(k)
            if e is not None:
                if e[0] is not None:
                    self._dep(op, e[0], raw=False)
                for r in e[1]:
                    self._dep(op, r, raw=False)
        for k in rk:
            e = res.get(k)
            if e is None:
                res[k] = [None, [op]]
            else:
                e[1].append(op)
        for k in wk:
            res[k] = [op, []]
        if dma:
            slot = self.dma_rr[eng]
            self.dma_rr[eng] = (slot + 1) % self.n_dma_sems
            prev = self.dma_last.get((eng, slot))
            if prev is not None:
                op.deps.add(prev)
            self.dma_last[(eng, slot)] = op
            op.sem = (eng, slot)
        self.ops.append(op)
        return op

    def _dep(self, op, d, raw):
        if d is op:
            return
        if (not d.is_dma) and (not op.is_dma) and d.eng == op.eng:
            if op.eng == "pe":
                return
        op.deps.add(d)

    def mark_final(self, op):
        self.final_ops.append(op)

    def emit(self):
        nc = self.nc
        ops = self.ops
        needed = set()
        for op in ops:
            for d in op.deps:
                needed.add(d.idx)
        for op in self.final_ops:
            needed.add(op.idx)
        cnt = {e: 0 for e in self.ENGS}
        dcnt = {}
        for op in ops:
            if op.is_dma:
                dcnt[op.sem] = dcnt.get(op.sem, 0) + 16
                op.sigval = dcnt[op.sem]
                op.signal = True
            elif op.idx in needed:
                cnt[op.eng] += 1
                op.sigval = cnt[op.eng]
                op.sem = op.eng
                op.signal = True
        per_eng = {e: [] for e in self.ENGS}
        for op in ops:
            per_eng[op.eng].append(op)
        final_ops = self.final_ops
        stats = {e: [len(per_eng[e]), 0] for e in self.ENGS}

        import contextlib
        with contextlib.ExitStack() as es:
            sems = {}
            for e in self.ENGS:
                sems[e] = es.enter_context(nc.semaphore("s_" + e))
            for q in ("sp", "pool", "act"):
                for s in range(self.n_dma_sems):
                    sems[(q, s)] = es.enter_context(nc.semaphore("d_%s_%d" % (q, s)))
            block = es.enter_context(nc.Block())

            def run_engine(engname, eng):
                waited = {}
                nw = 0
                for op in per_eng[engname]:
                    for d in sorted(op.deps, key=lambda o: o.idx):
                        key = d.sem
                        if waited.get(key, 0) >= d.sigval:
                            continue
                        eng.wait_ge(sems[key], d.sigval)
                        waited[key] = d.sigval
                        nw += 1
                    ins = op.fn(eng)
                    if op.signal:
                        ins.then_inc(sems[op.sem], 16 if op.is_dma else 1)
                if engname == "sp":
                    for op in final_ops:
                        key = op.sem
                        if waited.get(key, 0) < op.sigval:
                            eng.wait_ge(sems[key], op.sigval)
                            waited[key] = op.sigval
                stats[engname][1] = nw

            @block.sync
            def _(e):
                run_engine("sp", e)

            @block.gpsimd
            def _(e):
                run_engine("pool", e)

            @block.scalar
            def _(e):
                run_engine("act", e)

            @block.vector
            def _(e):
                run_engine("dve", e)

            @block.tensor
            def _(e):
                run_engine("pe", e)
        self.stats = stats
        return stats


S = 2048
D = 1024
NT = 4
NB = 16
DIN = 2816
DFF = 4096
PLE = 256
EPS = 1e-6
NSLOT = 5
LOOKAHEAD = 2
NVEC = 40


def make_consts():
    c = np.zeros((128, 7, 128), dtype=np.float32)
    i = np.arange(128)
    c[:, 0, :] = np.eye(128, dtype=np.float32)
    c[:, 1, :] = (i[:, None] < i[None, :]).astype(np.float32)
    c[:, 2, :] = (i[:, None] <= i[None, :]).astype(np.float32)
    c[:, 3, :] = 1.0 / 1024.0
    blk = np.zeros((128, 128), dtype=np.float32)
    blk[:64, :64] = 1.0 / 64.0
    blk[64:, 64:] = 1.0 / 64.0
    c[:, 4, :] = blk
    c[:, 5, :] = -(i[:, None] >= i[None, :]).astype(np.float32)
    c[:, 6, :] = -(i[:, None] < i[None, :]).astype(np.float32)
    return c


class _Stop(Exception):
    pass


def build(n_layers, upto=99):
    import contextlib
    nc = bass.Bass("TRN2", target_bir_lowering=False)
    L = n_layers

    def din(name, shape):
        return nc.dram_tensor(name, list(shape), F32, kind="ExternalInput").ap()

    x_d = din("x", [S, D])
    p_d = din("p", [L, S, PLE])
    n1_d = din("norm1_g", [L, D])
    win_d = din("w_in", [L, D, DIN])
    cw_d = din("conv_w", [L, 3, 256])
    qg_d = din("q_norm_g", [L, 64])
    kg_d = din("k_norm_g", [L, 64])
    sg_d = din("sgu_norm_g", [L, 256])
    sw_d = din("sgu_w", [L, 4, 128, 128])
    sb_d = din("sgu_b", [L, 4, 128])
    wout_d = din("w_out", [L, D, D])
    n2_d = din("norm2_g", [L, D])
    wf1_d = din("w_ff1", [L, D, DFF])
    wf2_d = din("w_ff2", [L, DFF, D])
    n3_d = din("norm3_g", [L, D])
    wg_d = din("w_ple_gate", [L, D, D])
    wp_d = din("w_ple_proj", [L, PLE, D])
    cst_d = din("cst", [128, 7, 128])
    out_d = nc.dram_tensor("out", [S, D], F32, kind="ExternalOutput").ap()

    es = contextlib.ExitStack()
    with es:
        def sb(name, shape, dt):
            return es.enter_context(nc.sbuf_tensor(name, list(shape), dt))

        h = [sb("h%d" % c, [128, S], F32) for c in range(8)]
        XY = [sb("xy%d" % c, [128, S], BF16) for c in range(8)]
        M = sb("M", [128, 32768], BF16)
        Wt = [sb("W%d" % s, [128, 2048], BF16) for s in range(NSLOT)]
        FT = sb("FT", [128, 4, 512], F32)
        BT = sb("BT", [128, 8, 512], BF16)
        G2 = sb("G2", [128, 2, 514], F32)
        vec = sb("vec", [128, L * NVEC], F32)
        bbt = sb("bbt", [128, 2, 128], F32)
        wmT = sb("wmT", [128, 4, 128], BF16)
        sst = sb("sst", [128, 64], F32)
        cF = sb("cF", [128, 3, 128], F32)
        cB = sb("cB", [128, 4, 128], BF16)
        banks = [es.enter_context(nc.psum_tensor("B%d" % i, [128, 512], F32)) for i in range(8)]

        ident = cF[:, 0, :]
        mask_strict = cF[:, 1, :]
        mask_incl = cF[:, 2, :]
        onesd = cB[:, 0, :]
        blockones = cB[:, 1, :]
        negTriUI = cB[:, 2, :]
        negTriL = cB[:, 3, :]

        qT = [M[:, c * 2048:(c + 1) * 2048] for c in range(4)]
        kT = [M[:, 8192 + c * 2048: 8192 + (c + 1) * 2048] for c in range(4)]
        vtok = [M[:, 16384 + tb * 512: 16384 + (tb + 1) * 512] for tb in range(NB)]
        ya = [M[:, 24576 + c * 2048: 24576 + (c + 1) * 2048] for c in range(2)]
        yc = [M[:, 28672 + c * 2048: 28672 + (c + 1) * 2048] for c in range(2)]
        fbuf = [M[:, c * 2048:(c + 1) * 2048] for c in range(16)]
        pT = [M[:, c * 2048:(c + 1) * 2048] for c in range(2)]
        gvb_all = M[:, 24576:28672]

        P = Prog(nc)

        rr = {"bank": 0, "ft": 0, "bt": 0}

        bank_cfg = {"lo": 0, "n": 6}

        def nbank():
            b = banks[bank_cfg["lo"] + rr["bank"] % bank_cfg["n"]]
            rr["bank"] += 1
            return b

        def ft():
            t = FT[:, rr["ft"] % 4, :]
            rr["ft"] += 1
            return t

        def bt():
            t = BT[:, rr["bt"] % 8, :]
            rr["bt"] += 1
            return t

        def tsl(tt):
            return slice(tt * 512, (tt + 1) * 512)

        def MM(out, lhsT, rhs, start, stop, **kw):
            P.add("pe", lambda e: e.matmul(out, lhsT, rhs, start=start, stop=stop, **kw),
                  reads=[lhsT, rhs], writes=[out])

        def ACT(out, in_, func, reads=None, **kw):
            rd = [in_] + (reads or [])
            P.add("act", lambda e: e.activation(out=out, in_=in_, func=func, **kw), reads=rd, writes=[out])

        def TT(eng, out, in0, in1, op):
            P.add(eng, lambda e: e.tensor_tensor(out=out, in0=in0, in1=in1, op=op), reads=[in0, in1], writes=[out])

        def STT(eng, out, in0, scalar, in1, op0, op1):
            rd = [in0, in1] + ([scalar] if not isinstance(scalar, float) else [])
            P.add(eng, lambda e: e.scalar_tensor_tensor(out=out, in0=in0, scalar=scalar, in1=in1, op0=op0, op1=op1),
                  reads=rd, writes=[out])

        def TS(eng, out, in0, scalar1, op0):
            rd = [in0] + ([scalar1] if not isinstance(scalar1, float) else [])
            P.add(eng, lambda e: e.tensor_scalar(out=out, in0=in0, scalar1=scalar1, scalar2=None, op0=op0),
                  reads=rd, writes=[out])

        def COPY(eng, out, in_):
            if eng == "act":
                P.add("act", lambda e: e.copy(out, in_), reads=[in_], writes=[out])
            else:
                P.add(eng, lambda e: e.tensor_copy(out, in_), reads=[in_], writes=[out])

        def DMA(q, out, in_, slow=False):
            kw = {"allow_slow_non_contiguous": True} if slow else {}
            return P.add(q, lambda e: e.dma_start(out=out, in_=in_, **kw), reads=[in_], writes=[out], dma=True)

        worder = []
        for l in range(L):
            for g in [3, 4, 5, 6, 7, 8, 10, 9, 1, 2, 0]:
                worder.append((("in", l, g), win_d[l, :, g * 256:(g + 1) * 256].rearrange("(k p) n -> p k n", p=128), 8, 256))
            for half in range(2):
                for g in range(4):
                    worder.append((("out", l, half, g), wout_d[l, :, g * 256:(g + 1) * 256].rearrange("(k p) n -> p k n", p=128), 8, 256))
                for e in range(8):
                    for g in range(2):
                        gg = 2 * e + g
                        worder.append((("ff1", l, half, gg), wf1_d[l, :, gg * 256:(gg + 1) * 256].rearrange("(k p) n -> p k n", p=128), 8, 256))
                    for ch in range(2):
                        worder.append((("ff2", l, half, e, ch),
                                       wf2_d[l, e * 512:(e + 1) * 512, ch * 512:(ch + 1) * 512].rearrange("(k p) n -> p k n", p=128), 4, 512))
            for g in range(4):
                worder.append((("gate", l, g), wg_d[l, :, g * 256:(g + 1) * 256].rearrange("(k p) n -> p k n", p=128), 8, 256))
                worder.append((("proj", l, g), wp_d[l, :, g * 256:(g + 1) * 256].rearrange("(k p) n -> p k n", p=128), 2, 256))
        wstate = {"next_load": 0, "next_get": 0}

        def wview(i):
            k, n = worder[i][2], worder[i][3]
            return Wt[i % NSLOT][:, 0:k * n].rearrange("p (k n) -> p k n", k=k)

        def wget(key):
            i = wstate["next_get"]
            assert worder[i][0] == key, (worder[i][0], key)
            wstate["next_get"] += 1
            while wstate["next_load"] < len(worder) and wstate["next_load"] <= i + LOOKAHEAD:
                j = wstate["next_load"]
                DMA("pool", wview(j), worder[j][1])
                wstate["next_load"] += 1
            return wview(i)

        DMA("sp", cF[:], cst_d[:, 0:3, :])
        DMA("pool", cB[:], cst_d[:, 3:7, :])

        def vcol(l, k):
            return vec[:, l * NVEC + k: l * NVEC + k + 1]

        V_G1, V_G2, V_G3, V_CW, V_GQ, V_GK, V_SG = 0, 8, 16, 24, 30, 31, 32
        def load_vecs(l, q):
            b = l * NVEC
            DMA(q, vec[:, b + V_G1: b + V_G1 + 8], n1_d[l].rearrange("(c p) -> p c", p=128), slow=True)
            DMA(q, vec[:, b + V_G2: b + V_G2 + 8], n2_d[l].rearrange("(c p) -> p c", p=128), slow=True)
            DMA(q, vec[:, b + V_G3: b + V_G3 + 8], n3_d[l].rearrange("(c p) -> p c", p=128), slow=True)
            DMA(q, vec[:, b + V_CW: b + V_CW + 6].rearrange("p (j c) -> p j c", j=3),
                cw_d[l].rearrange("j (c p) -> p j c", p=128), slow=True)
            for hh in range(2):
                DMA(q, vec[hh * 64:(hh + 1) * 64, b + V_GQ: b + V_GQ + 1], qg_d[l].rearrange("(p o) -> p o", o=1), slow=True)
                DMA(q, vec[hh * 64:(hh + 1) * 64, b + V_GK: b + V_GK + 1], kg_d[l].rearrange("(p o) -> p o", o=1), slow=True)
            DMA(q, vec[:, b + V_SG: b + V_SG + 2], sg_d[l].rearrange("(c p) -> p c", p=128), slow=True)

        def scale_vecs(l):
            b = l * NVEC
            TS("dve", vcol(l, V_GQ), vcol(l, V_GQ), 0.125, ALU.mult)
            TS("dve", vec[:, b + V_SG: b + V_SG + 2], vec[:, b + V_SG: b + V_SG + 2], 8.0, ALU.mult)


        if upto >= 0:
            load_vecs(0, "pool")

        stg = [FT[:, 0:2, :], FT[:, 2:4, :]]
        for tb in range(NB):
            st = stg[tb % 2]
            DMA("sp", st, x_d[tb * 128:(tb + 1) * 128, :].rearrange("p (a b) -> p a b", a=2))
            for half in range(2):
                bk = banks[6 + half]
                for cc in range(4):
                    c = half * 4 + cc
                    src = st[:, c // 4, (c % 4) * 128:(c % 4 + 1) * 128]
                    dst = bk[:, cc * 128:(cc + 1) * 128]
                    P.add("pe", lambda e, dst=dst, src=src: e.transpose(dst, src, ident), reads=[src, ident], writes=[dst])
                for cc in range(4):
                    c = half * 4 + cc
                    COPY("dve" if cc % 2 == 0 else "act", h[c][:, tb * 128:(tb + 1) * 128], bk[:, cc * 128:(cc + 1) * 128])

        def rmsnorm(l, vbase, tts=None, talloc=None):
            bks = {}
            tts = list(range(NT)) if tts is None else list(tts)
            talloc = ft if talloc is None else talloc

            def sq_mm(tt):
                ts_ = tsl(tt)
                for c in range(8):
                    ACT(XY[c][:, ts_], h[c][:, ts_], AF.Square)
                bk = nbank()
                for c in range(8):
                    MM(bk[:], onesd, XY[c][:, ts_], c == 0, c == 7)
                bks[tt] = bk

            def fin(tt):
                ts_ = tsl(tt)
                r = talloc()
                ACT(r, bks[tt][:], AF.Ln, bias=EPS)
                ACT(r, r, AF.Exp, scale=-0.5)
                for c in range(8):
                    STT("dve", XY[c][:, ts_], h[c][:, ts_], vcol(l, vbase + c), r, ALU.mult, ALU.mult)

            sq_mm(tts[0])
            for k in range(1, len(tts)):
                sq_mm(tts[k])
                fin(tts[k - 1])
            fin(tts[-1])

        def dense_fm(wv, oc, tt, rhs_list, kcs=None):
            bk = nbank()
            n = len(rhs_list)
            for i in range(n):
                kc = i if kcs is None else kcs[i]
                MM(bk[:], wv[:, kc, oc * 128:(oc + 1) * 128], rhs_list[i][:, tsl(tt)], i == 0, i == n - 1)
            return bk

        def chk(k):
            if upto < k:
                raise _Stop()

        for l in range(L):
          try:
              chk(1)
              scale_vecs(l)
              if l + 1 < L:
                  load_vecs(l + 1, "sp")
              for g in range(4):
                  DMA("sp", bbt[(g % 2) * 64:(g % 2) * 64 + 64, g // 2, :], sb_d[l, g:g + 1, :].broadcast_to([64, 128]))
              swst = ft()
              DMA("sp", swst.rearrange("p (g s) -> p g s", g=4), sw_d[l].rearrange("g t s -> t g s"))
              bk = banks[6]
              for g in range(4):
                  src = swst[:, g * 128:(g + 1) * 128]
                  dst = bk[:, g * 128:(g + 1) * 128]
                  P.add("pe", lambda e, dst=dst, src=src: e.transpose(dst, src, ident), reads=[src, ident], writes=[dst])
              for g in range(4):
                  TT("dve", wmT[:, g, :], bk[:, g * 128:(g + 1) * 128], mask_incl, ALU.mult)

              rmsnorm(l, V_G1)

              chk(2)
              qk_pend = [None]
              for which, dstT, gcol in (("q", qT, V_GQ), ("k", kT, V_GK)):
                  for gi in range(2):
                      wv = wget(("in", l, (3 if which == "q" else 5) + gi))
                      for oc in range(2):
                          qc = gi * 2 + oc
                          for tt in range(NT):
                              bk = dense_fm(wv, oc, tt, XY)
                              sq = bt()
                              ACT(sq, bk[:], AF.Square)
                              if qk_pend[0] is not None:
                                  qk_pend[0]()

                              def fin(bk=bk, sq=sq, dst=dstT[qc][:, tsl(tt)], gcol=gcol):
                                  b2 = nbank()
                                  MM(b2[:], blockones, sq, True, True)
                                  r = ft()
                                  ACT(r, b2[:], AF.Ln, bias=EPS)
                                  ACT(r, r, AF.Exp, scale=-0.5)
                                  STT("dve", dst, bk[:], vcol(l, gcol), r, ALU.mult, ALU.mult)
                              qk_pend[0] = fin
              if qk_pend[0] is not None:
                  qk_pend[0]()
                  qk_pend[0] = None
              chk(3)
              for gi in range(2):
                  wv = wget(("in", l, 7 + gi))
                  for tb in range(NB):
                      bk = nbank()
                      for kc in range(8):
                          MM(bk[:, 0:256], XY[kc][:, tb * 128:(tb + 1) * 128], wv[:, kc, :], kc == 0, kc == 7)
                      COPY("dve" if tb % 2 == 0 else "act", vtok[tb][:, gi * 256:(gi + 1) * 256], bk[:, 0:256])
              wv = wget(("in", l, 10))
              for tb in range(NB):
                  bk = nbank()
                  for kc in range(8):
                      MM(bk[:, 0:256], XY[kc][:, tb * 128:(tb + 1) * 128], wv[:, kc, :], kc == 0, kc == 7)
                  ACT(gvb_all[:, tb * 256:(tb + 1) * 256], bk[:, 0:256], AF.Gelu)
              sqall = BT[:].rearrange("p a b -> p (a b)")
              TT("dve", sqall, gvb_all, gvb_all, ALU.mult)
              P.add("dve", lambda e: e.tensor_reduce(out=sst[:], in_=sqall.rearrange("p (g e) -> p g e", e=64), axis=AX.X, op=ALU.add),
                    reads=[sqall], writes=[sst[:]])
              ACT(sst[:], sst[:], AF.Ln, bias=64.0 * EPS)
              ACT(sst[:], sst[:], AF.Exp, scale=-0.5)
              gv3 = gvb_all.rearrange("p (g e) -> p g e", e=64)
              TT("dve", gv3, gv3, sst[:].unsqueeze(2).broadcast_to([128, 64, 64]), ALU.mult)
              wv = wget(("in", l, 9))
              for oc in range(2):
                  for tt in range(NT):
                      bk = dense_fm(wv, oc, tt, XY)
                      ACT(yc[oc][:, tsl(tt)], bk[:], AF.Gelu)
              for tb2 in range(NB // 2):
                  bk = nbank()
                  for t2 in range(2):
                      tb = tb2 * 2 + t2
                      for g in range(4):
                          gp, gh = g // 2, g % 2
                          o = bk[gh * 64:(gh + 1) * 64, (t2 * 2 + gp) * 128:(t2 * 2 + gp + 1) * 128]
                          lhs = gvb_all[:, tb * 256 + g * 64: tb * 256 + (g + 1) * 64]
                          kw = {"tile_position": (0, 64)} if gh == 1 else {}
                          MM(o, lhs, wmT[:, g, :], True, True, **kw)
                  for t2 in range(2):
                      tb = tb2 * 2 + t2
                      for gp in range(2):
                          tmp = ft()[:, 0:128]
                          STT("dve", tmp, bk[:, (t2 * 2 + gp) * 128:(t2 * 2 + gp + 1) * 128],
                              vcol(l, V_SG + gp), bbt[:, gp, :], ALU.mult, ALU.add)
                          TT("dve", yc[gp][:, tb * 128:(tb + 1) * 128], tmp, yc[gp][:, tb * 128:(tb + 1) * 128], ALU.mult)
              chk(4)
              wc = wget(("in", l, 1))
              wh = wget(("in", l, 2))
              wb = wget(("in", l, 0))
              for oc in range(2):
                  for tt in range(NT):
                      gcur = G2[:, tt % 2, :]
                      gprev = G2[:, (tt + 1) % 2, :]
                      bk1 = dense_fm(wc, oc, tt, XY)
                      tc_ = ft()
                      COPY("act", tc_, bk1[:])
                      bk2 = dense_fm(wh, oc, tt, XY)
                      TT("dve", gcur[:, 2:514], tc_, bk2[:], ALU.mult)
                      if tt == 0:
                          P.add("dve", lambda e, gcur=gcur: e.memset(gcur[:, 0:2], 0.0), writes=[gcur[:, 0:2]])
                      else:
                          COPY("dve", gcur[:, 0:2], gprev[:, 512:514])
                      o = ft()
                      TS("dve", o, gcur[:, 2:514], vcol(l, V_CW + 2 * 2 + oc), ALU.mult)
                      STT("dve", o, gcur[:, 1:513], vcol(l, V_CW + 1 * 2 + oc), o, ALU.mult, ALU.add)
                      STT("dve", o, gcur[:, 0:512], vcol(l, V_CW + 0 * 2 + oc), o, ALU.mult, ALU.add)
                      bk3 = dense_fm(wb, oc, tt, XY)
                      TT("dve", ya[oc][:, tsl(tt)], bk3[:], o, ALU.mult)

              chk(5)

              def run_attention(steps, pbank, obank_of, per_iter=None):
                  nst = len(steps)

                  def geom(i):
                      hp, hh, tt, kb, first, last, sp_ = steps[i]
                      j = kb - 4 * tt
                      c0 = j * 128 if j >= 0 else 0
                      return hp, hh, tt, kb, first, last, sp_, j, c0, slice(hh * 64, (hh + 1) * 64)

                  def pe_z(i):
                      hp, hh, tt, kb, first, last, sp_, j, c0, rows = geom(i)
                      zb = banks[i % 2]
                      MM(zb[:, c0:512], kT[hp][rows, kb * 128:(kb + 1) * 128], qT[hp][rows, tt * 512 + c0:(tt + 1) * 512], True, True)

                  def act_1(i):
                      hp, hh, tt, kb, first, last, sp_, j, c0, rows = geom(i)
                      zb = banks[i % 2]
                      E = FT[:, i % 4, :]
                      Lp = BT[:, i % 4, :]
                      ACT(E[:, c0:512], zb[:, c0:512], AF.Exp)
                      if j >= 0:
                          TT("dve", E[:, c0:c0 + 128], E[:, c0:c0 + 128], mask_strict, ALU.mult)
                          if c0 + 128 < 512:
                              ACT(Lp[:, c0 + 128:512], E[:, c0 + 128:512], AF.Ln, bias=1.0)
                          ACT(Lp[:, c0:c0 + 128], E[:, c0:c0 + 128], AF.Ln, bias=1.0)
                      else:
                          ACT(Lp[:, c0:512], E[:, c0:512], AF.Ln, bias=1.0)

                  def pe_tri(i):
                      hp, hh, tt, kb, first, last, sp_, j, c0, rows = geom(i)
                      pb = pbank(sp_)
                      Lp = BT[:, i % 4, :]
                      MM(pb[:, c0:512], negTriUI, Lp[:, c0:512], first, True, skip_group_check=True)

                  def act_2(i):
                      hp, hh, tt, kb, first, last, sp_, j, c0, rows = geom(i)
                      pb = pbank(sp_)
                      E = FT[:, i % 4, :]
                      X = BT[:, 4 + i % 2, :]
                      A = BT[:, 6 + i % 2, :]
                      ACT(X[:, c0:512], pb[:, c0:512], AF.Exp)
                      TT("dve", A[:, c0:512], E[:, c0:512], X[:, c0:512], ALU.mult)

                  def pe_3(i):
                      hp, hh, tt, kb, first, last, sp_, j, c0, rows = geom(i)
                      pb = pbank(sp_)
                      obank = obank_of(sp_)
                      Lp = BT[:, i % 4, :]
                      A = BT[:, 6 + i % 2, :]
                      kw = {"tile_position": (0, 64)} if hh == 1 else {}
                      MM(obank[rows, c0:512], vtok[kb][:, (2 * hp + hh) * 64:(2 * hp + hh + 1) * 64], A[:, c0:512],
                         first, last, skip_group_check=True, **kw)
                      if not last:
                          MM(pb[:, c0:512], negTriL, Lp[:, c0:512], False, True, skip_group_check=True)
                      else:
                          COPY("dve", qT[hp][rows, tsl(tt)], obank[rows, :])

                  for t in range(nst + 5):
                      if 0 <= t - 4 < nst:
                          pe_3(t - 4)
                      if 0 <= t - 2 < nst:
                          pe_tri(t - 2)
                      if 0 <= t < nst:
                          pe_z(t)
                      if 0 <= t - 3 < nst:
                          act_2(t - 3)
                      if 0 <= t - 1 < nst:
                          act_1(t - 1)
                      if per_iter is not None:
                          per_iter(t)

              def dense_half(half):
                  tts = [0, 1] if half == 0 else [2, 3]
                  oth = slice(1024, 2048) if half == 0 else slice(0, 1024)
                  fX = [XY[c][:, oth] for c in range(4)]
                  ntmp = 2 if half == 0 else 4
                  tmps = [XY[4 + k][:, oth].bitcast(F32) for k in range(ntmp)]
                  tstate = [0]

                  def talloc():
                      t = tmps[tstate[0] % ntmp]
                      tstate[0] += 1
                      return t

                  def ltsl(tt):
                      lt = tt - tts[0]
                      return slice(lt * 512, (lt + 1) * 512)

                  ysrc = [ya[0], ya[1], qT[0], qT[1], qT[2], qT[3], yc[0], yc[1]]
                  for g in range(4):
                      wv = wget(("out", l, half, g))
                      for oc in range(2):
                          o = g * 2 + oc
                          for tt in tts:
                              bk = dense_fm(wv, oc, tt, ysrc)
                              TT("dve", h[o][:, tsl(tt)], h[o][:, tsl(tt)], bk[:], ALU.add)
                              yield
                  rmsnorm(l, V_G2, tts, talloc)
                  yield
                  for e in range(8):
                      for g in range(2):
                          wv = wget(("ff1", l, half, 2 * e + g))
                          for oc in range(2):
                              fc = g * 2 + oc
                              for tt in tts:
                                  bk = dense_fm(wv, oc, tt, XY)
                                  t1 = talloc()
                                  ACT(t1, bk[:], AF.Relu)
                                  TT("dve", fX[fc][:, ltsl(tt)], t1, t1, ALU.mult)
                                  yield
                      for ch in range(2):
                          wv = wget(("ff2", l, half, e, ch))
                          for oc4 in range(4):
                              o = ch * 4 + oc4
                              for tt in tts:
                                  bk = nbank()
                                  for kc in range(4):
                                      MM(bk[:], wv[:, kc, oc4 * 128:(oc4 + 1) * 128], fX[kc][:, ltsl(tt)], kc == 0, kc == 3)
                                  TT("dve", h[o][:, tsl(tt)], h[o][:, tsl(tt)], bk[:], ALU.add)
                                  yield

              stepsA = []
              for pg in range(2):
                  for tt in (0, 1):
                      kmax = 4 * tt + 3
                      for kb in range(kmax, -1, -1):
                          for sp_ in range(4):
                              stepsA.append((2 * pg + sp_ // 2, sp_ % 2, tt, kb, kb == kmax, kb == 0, sp_))
              run_attention(stepsA, lambda sp_: banks[2 + sp_], lambda sp_: banks[6 + sp_ // 2])

              chk(6)
              stepsB = []
              for hp in range(4):
                  for tt in (2, 3):
                      kmax = 4 * tt + 3
                      for kb in range(kmax, -1, -1):
                          for hh in range(2):
                              stepsB.append((hp, hh, tt, kb, kb == kmax, kb == 0, hh))
              genA = dense_half(0)
              bank_cfg["lo"], bank_cfg["n"] = 5, 3

              def pull(t):
                  next(genA, None)
                  next(genA, None)

              def run_attention_pairs(pairs, per_iter):
                  npair = len(pairs)
                  Abuf = [XY[6 + k // 2][:, 1024 + (k % 2) * 512: 1024 + (k % 2 + 1) * 512] for k in range(4)]

                  def geom(u, hh):
                      hp, tt, kb, first, last = pairs[u]
                      j = kb - 4 * tt
                      c0 = j * 128 if j >= 0 else 0
                      return hp, tt, kb, first, last, j, c0, slice(hh * 64, (hh + 1) * 64)

                  def pe_z(u):
                      for hh in range(2):
                          hp, tt, kb, first, last, j, c0, rows = geom(u, hh)
                          zb = banks[hh]
                          MM(zb[:, c0:512], kT[hp][rows, kb * 128:(kb + 1) * 128], qT[hp][rows, tt * 512 + c0:(tt + 1) * 512], True, True)

                  def act_1(u):
                      for hh in range(2):
                          hp, tt, kb, first, last, j, c0, rows = geom(u, hh)
                          zb = banks[hh]
                          E = FT[:, (2 * u + hh) % 4, :]
                          Lp = BT[:, (2 * u + hh) % 6, :]
                          ACT(E[:, c0:512], zb[:, c0:512], AF.Exp)
                          if j >= 0:
                              TT("dve", E[:, c0:c0 + 128], E[:, c0:c0 + 128], mask_strict, ALU.mult)
                              if c0 + 128 < 512:
                                  ACT(Lp[:, c0 + 128:512], E[:, c0 + 128:512], AF.Ln, bias=1.0)
                              ACT(Lp[:, c0:c0 + 128], E[:, c0:c0 + 128], AF.Ln, bias=1.0)
                          else:
                              ACT(Lp[:, c0:512], E[:, c0:512], AF.Ln, bias=1.0)

                  def pe_tri(u):
                      for hh in range(2):
                          hp, tt, kb, first, last, j, c0, rows = geom(u, hh)
                          Lp = BT[:, (2 * u + hh) % 6, :]
                          MM(banks[2 + hh][:, c0:512], negTriUI, Lp[:, c0:512], first, True, skip_group_check=True)

                  def act_2(u):
                      for hh in range(2):
                          hp, tt, kb, first, last, j, c0, rows = geom(u, hh)
                          E = FT[:, (2 * u + hh) % 4, :]
                          X = BT[:, 6 + hh, :]
                          A = Abuf[(2 * u + hh) % 4]
                          ACT(X[:, c0:512], banks[2 + hh][:, c0:512], AF.Exp)
                          TT("dve", A[:, c0:512], E[:, c0:512], X[:, c0:512], ALU.mult)

                  def pe_3(u):
                      obank = banks[4]
                      for hh in range(2):
                          hp, tt, kb, first, last, j, c0, rows = geom(u, hh)
                          A = Abuf[(2 * u + hh) % 4]
                          kw = {"tile_position": (0, 64)} if hh == 1 else {}
                          MM(obank[rows, c0:512], vtok[kb][:, (2 * hp + hh) * 64:(2 * hp + hh + 1) * 64], A[:, c0:512],
                             first, last, skip_group_check=True, **kw)
                      for hh in range(2):
                          hp, tt, kb, first, last, j, c0, rows = geom(u, hh)
                          Lp = BT[:, (2 * u + hh) % 6, :]
                          if not last:
                              MM(banks[2 + hh][:, c0:512], negTriL, Lp[:, c0:512], False, True, skip_group_check=True)
                          else:
                              COPY("dve", qT[hp][rows, tsl(tt)], obank[rows, :])

                  for u in range(npair + 2):
                      if 0 <= u - 2 < npair:
                          pe_3(u - 2)
                      if 0 <= u - 1 < npair:
                          pe_tri(u - 1)
                      if 0 <= u < npair:
                          pe_z(u)
                      if 0 <= u - 1 < npair:
                          act_2(u - 1)
                      if 0 <= u < npair:
                          act_1(u)
                      per_iter(u)

              pairsB = []
              for hp in range(4):
                  for tt in (2, 3):
                      kmax = 4 * tt + 3
                      for kb in range(kmax, -1, -1):
                          pairsB.append((hp, tt, kb, kb == kmax, kb == 0))
              run_attention_pairs(pairsB, pull)
              for _ in genA:
                  pass
              bank_cfg["lo"], bank_cfg["n"] = 0, 6
              for _ in dense_half(1):
                  pass

              chk(8)
              for q4 in range(4):
                  st_ = stg[q4 % 2]
                  DMA("sp", st_.rearrange("p a b -> p (a b)").rearrange("p (t f) -> p t f", t=4),
                      p_d[l, q4 * 512:(q4 + 1) * 512, :].rearrange("(t p) f -> p t f", p=128))
                  for fc in range(2):
                      bk = banks[6 + fc]
                      for t4 in range(4):
                          e0 = t4 * 256 + fc * 128
                          src = st_[:, e0 // 512, (e0 % 512):(e0 % 512) + 128]
                          dst = bk[:, t4 * 128:(t4 + 1) * 128]
                          P.add("pe", lambda e, dst=dst, src=src: e.transpose(dst, src, ident), reads=[src, ident], writes=[dst])
                      COPY("dve" if fc == 0 else "act", pT[fc][:, tsl(q4)], bk[:])
              rmsnorm(l, V_G3)
              for g in range(4):
                  wv = wget(("gate", l, g))
                  wpv = wget(("proj", l, g))
                  for oc in range(2):
                      o = g * 2 + oc
                      for tt in range(NT):
                          bk = dense_fm(wv, oc, tt, XY)
                          gt = ft()
                          ACT(gt, bk[:], AF.Sigmoid)
                          bk2 = nbank()
                          for fc in range(2):
                              MM(bk2[:], wpv[:, fc, oc * 128:(oc + 1) * 128], pT[fc][:, tsl(tt)], fc == 0, fc == 1)
                          TT("dve", gt, gt, bk2[:], ALU.mult)
                          TT("dve", h[o][:, tsl(tt)], h[o][:, tsl(tt)], gt, ALU.add)

          except _Stop:
              break

        for tb in range(NB):
            st = stg[tb % 2]
            for half in range(2):
                bk = banks[6 + half]
                for cc in range(4):
                    c = half * 4 + cc
                    src = h[c][:, tb * 128:(tb + 1) * 128]
                    dst = bk[:, cc * 128:(cc + 1) * 128]
                    P.add("pe", lambda e, dst=dst, src=src: e.transpose(dst, src, ident), reads=[src, ident], writes=[dst])
                COPY("dve" if half == 0 else "act", st[:, half, :], bk[:])
            op = DMA("sp", out_d[tb * 128:(tb + 1) * 128, :].rearrange("p (a b) -> p a b", a=2), st)
            P.mark_final(op)

        assert upto < 99 or wstate["next_get"] == len(worder), (wstate, len(worder))
        stats = P.emit()
    return nc, stats


WNAMES = ["norm1_g", "w_in", "conv_w", "q_norm_g", "k_norm_g", "sgu_norm_g", "sgu_w", "sgu_b", "w_out",
          "norm2_g", "w_ff1", "w_ff2", "norm3_g", "w_ple_gate", "w_ple_proj"]

_CACHE = {}


def _get_prog(n_layers):
    if n_layers not in _CACHE:
        _CACHE[n_layers] = build(n_layers)[0]
    return _CACHE[n_layers]


def kernel(**inputs):
    x = np.ascontiguousarray(np.asarray(inputs["x"], dtype=np.float32))
    p = np.asarray(inputs["p"], dtype=np.float32)
    depth = p.shape[0]
    nb = x.shape[0]
    cst = make_consts()
    nc = _get_prog(depth)
    wts = {k: np.ascontiguousarray(np.asarray(inputs[k], dtype=np.float32)) for k in WNAMES}
    in_maps = []
    for b in range(nb):
        m = {"x": x[b], "p": np.ascontiguousarray(p[:, b]), "cst": cst}
        m.update(wts)
        in_maps.append(m)
    res = run_bass_kernel_spmd(nc, in_maps, core_ids=list(range(nb)))
    out = np.stack([np.asarray(r["out"], dtype=np.float32) for r in res.results], axis=0)
    return out
```

```python
import numpy as np
import concourse.bass as bass
import concourse.mybir as mybir
from concourse.bass_utils import run_bass_kernel_spmd

F32 = mybir.dt.float32
BF16 = mybir.dt.bfloat16
AF = mybir.ActivationFunctionType
ALU = mybir.AluOpType
AX = mybir.AxisListType

DSZ = {F32: 4, BF16: 2}
BLKB = 256


def _dsize(dt):
    for k, v in DSZ.items():
        if dt == k:
            return v
    raise ValueError(str(dt))


class Op:
    __slots__ = ("eng", "fn", "deps", "is_dma", "sem", "sigval", "signal", "idx", "raw_same")

    def __init__(self, eng, fn, is_dma):
        self.eng = eng
        self.fn = fn
        self.is_dma = is_dma
        self.deps = set()
        self.sem = None
        self.sigval = None
        self.signal = False


class Prog:
    ENGS = ("pe", "act", "dve", "pool", "sp")

    def __init__(self, nc, n_dma_sems=8):
        self.nc = nc
        self.ops = []
        self.res = {}
        self.n_dma_sems = n_dma_sems
        self.dma_rr = {"sp": 0, "pool": 0, "act": 0}
        self.dma_last = {}
        self.final_ops = []

    def _keys(self, ap):
        t = ap.tensor
        tn = type(t).__name__
        if tn.startswith("DRam"):
            return ()
        sz = _dsize(ap.dtype)
        pat = ap.ap
        pstride, npart = pat[0]
        off = ap.offset
        if pstride == 0:
            pstride = 1 << 30
        p0 = off // pstride
        f0 = (off % pstride) * sz
        ext = 0
        for st, cnt in pat[1:]:
            ext += (cnt - 1) * abs(st)
        f1 = f0 + ext * sz + sz - 1
        name = t.name
        if tn.startswith("PSum"):
            return [(name, pg, -1) for pg in range(p0 // 64, (p0 + npart - 1) // 64 + 1)]
        keys = []
        for pg in range(p0 // 64, (p0 + npart - 1) // 64 + 1):
            for b in range(f0 // BLKB, f1 // BLKB + 1):
                keys.append((name, pg, b))
        return keys

    def add(self, eng, fn, reads=(), writes=(), dma=False):
        op = Op(eng, fn, dma)
        op.idx = len(self.ops)
        rk = []
        for ap in reads:
            rk.extend(self._keys(ap))
        wk = []
        for ap in writes:
            wk.extend(self._keys(ap))
        wk.extend(k for k in rk if k[2] == -1)
        rk = [k for k in rk if k[2] != -1]
        res = self.res
        for k in rk:
            e = res.get(k)
            if e is not None and e[0] is not None:
                self._dep(op, e[0], raw=True)
        for k in wk:
            e = res.get(k)
            if e is not None:
                if e[0] is not None:
                    self._dep(op, e[0], raw=False)
                for r in e[1]:
                    self._dep(op, r, raw=False)
        for k in rk:
            e = res.get(k)
            if e is None:
                res[k] = [None, [op]]
            else:
                e[1].append(op)
        for k in wk:
            res[k] = [op, []]
        if dma:
            slot = self.dma_rr[eng]
            self.dma_rr[eng] = (slot + 1) % self.n_dma_sems
            prev = self.dma_last.get((eng, slot))
            if prev is not None:
                op.deps.add(prev)
            self.dma_last[(eng, slot)] = op
            op.sem = (eng, slot)
        self.ops.append(op)
        return op

    def _dep(self, op, d, raw):
        if d is op:
            return
        if (not d.is_dma) and (not op.is_dma) and d.eng == op.eng:
            if op.eng == "pe":
                return
        op.deps.add(d)

    def mark_final(self, op):
        self.final_ops.append(op)

    def emit(self):
        nc = self.nc
        ops = self.ops
        needed = set()
        for op in ops:
            for d in op.deps:
                needed.add(d.idx)
        for op in self.final_ops:
            needed.add(op.idx)
        cnt = {e: 0 for e in self.ENGS}
        dcnt = {}
        for op in ops:
            if op.is_dma:
                dcnt[op.sem] = dcnt.get(op.sem, 0) + 16
                op.sigval = dcnt[op.sem]
                op.signal = True
            elif op.idx in needed:
                cnt[op.eng] += 1
                op.sigval = cnt[op.eng]
                op.sem = op.eng
                op.signal = True
        per_eng = {e: [] for e in self.ENGS}
        for op in ops:
            per_eng[op.eng].append(op)
        final_ops = self.final_ops
        stats = {e: [len(per_eng[e]), 0] for e in self.ENGS}

        import contextlib
        with contextlib.ExitStack() as es:
            sems = {}
            for e in self.ENGS:
                sems[e] = es.enter_context(nc.semaphore("s_" + e))
            for q in ("sp", "pool", "act"):
                for s in range(self.n_dma_sems):
                    sems[(q, s)] = es.enter_context(nc.semaphore("d_%s_%d" % (q, s)))
            block = es.enter_context(nc.Block())

            def run_engine(engname, eng):
                waited = {}
                nw = 0
                for op in per_eng[engname]:
                    for d in sorted(op.deps, key=lambda o: o.idx):
                        key = d.sem
                        if waited.get(key, 0) >= d.sigval:
                            continue
                        eng.wait_ge(sems[key], d.sigval)
                        waited[key] = d.sigval
                        nw += 1
                    ins = op.fn(eng)
                    if op.signal:
                        ins.then_inc(sems[op.sem], 16 if op.is_dma else 1)
                if engname == "sp":
                    for op in final_ops:
                        key = op.sem
                        if waited.get(key, 0) < op.sigval:
                            eng.wait_ge(sems[key], op.sigval)
                            waited[key] = op.sigval
                stats[engname][1] = nw

            @block.sync
            def _(e):
                run_engine("sp", e)

            @block.gpsimd
            def _(e):
                run_engine("pool", e)

            @block.scalar
            def _(e):
                run_engine("act", e)

            @block.vector
            def _(e):
                run_engine("dve", e)

            @block.tensor
            def _(e):
                run_engine("pe", e)
        self.stats = stats
        return stats


S = 2048
D = 1024
NT = 4
NB = 16
DIN = 2816
DFF = 4096
PLE = 256
EPS = 1e-6
NSLOT = 5
LOOKAHEAD = 2
NVEC = 40


def make_consts():
    c = np.zeros((128, 7, 128), dtype=np.float32)
    i = np.arange(128)
    c[:, 0, :] = np.eye(128, dtype=np.float32)
    c[:, 1, :] = (i[:, None] < i[None, :]).astype(np.float32)
    c[:, 2, :] = (i[:, None] <= i[None, :]).astype(np.float32)
    c[:, 3, :] = 1.0 / 1024.0
    blk = np.zeros((128, 128), dtype=np.float32)
    blk[:64, :64] = 1.0 / 64.0
    blk[64:, 64:] = 1.0 / 64.0
    c[:, 4, :] = blk
    c[:, 5, :] = -(i[:, None] >= i[None, :]).astype(np.float32)
    c[:, 6, :] = -(i[:, None] < i[None, :]).astype(np.float32)
    return c


class _Stop(Exception):
    pass


def build(n_layers, upto=99):
    import contextlib
    nc = bass.Bass("TRN2", target_bir_lowering=False)
    L = n_layers

    def din(name, shape):
        return nc.dram_tensor(name, list(shape), F32, kind="ExternalInput").ap()

    x_d = din("x", [S, D])
    p_d = din("p", [L, S, PLE])
    n1_d = din("norm1_g", [L, D])
    win_d = din("w_in", [L, D, DIN])
    cw_d = din("conv_w", [L, 3, 256])
    qg_d = din("q_norm_g", [L, 64])
    kg_d = din("k_norm_g", [L, 64])
    sg_d = din("sgu_norm_g", [L, 256])
    sw_d = din("sgu_w", [L, 4, 128, 128])
    sb_d = din("sgu_b", [L, 4, 128])
    wout_d = din("w_out", [L, D, D])
    n2_d = din("norm2_g", [L, D])
    wf1_d = din("w_ff1", [L, D, DFF])
    wf2_d = din("w_ff2", [L, DFF, D])
    n3_d = din("norm3_g", [L, D])
    wg_d = din("w_ple_gate", [L, D, D])
    wp_d = din("w_ple_proj", [L, PLE, D])
    cst_d = din("cst", [128, 7, 128])
    out_d = nc.dram_tensor("out", [S, D], F32, kind="ExternalOutput").ap()

    es = contextlib.ExitStack()
    with es:
        def sb(name, shape, dt):
            return es.enter_context(nc.sbuf_tensor(name, list(shape), dt))

        h = [sb("h%d" % c, [128, S], F32) for c in range(8)]
        XY = [sb("xy%d" % c, [128, S], BF16) for c in range(8)]
        M = sb("M", [128, 32768], BF16)
        Wt = [sb("W%d" % s, [128, 2048], BF16) for s in range(NSLOT)]
        FT = sb("FT", [128, 4, 512], F32)
        BT = sb("BT", [128, 8, 512], BF16)
        G2 = sb("G2", [128, 2, 514], F32)
        vec = sb("vec", [128, L * NVEC], F32)
        bbt = sb("bbt", [128, 2, 128], F32)
        wmT = sb("wmT", [128, 4, 128], BF16)
        sst = sb("sst", [128, 64], F32)
        cF = sb("cF", [128, 3, 128], F32)
        cB = sb("cB", [128, 4, 128], BF16)
        banks = [es.enter_context(nc.psum_tensor("B%d" % i, [128, 512], F32)) for i in range(8)]

        ident = cF[:, 0, :]
        mask_strict = cF[:, 1, :]
        mask_incl = cF[:, 2, :]
        onesd = cB[:, 0, :]
        blockones = cB[:, 1, :]
        negTriUI = cB[:, 2, :]
        negTriL = cB[:, 3, :]

        qT = [M[:, c * 2048:(c + 1) * 2048] for c in range(4)]
        kT = [M[:, 8192 + c * 2048: 8192 + (c + 1) * 2048] for c in range(4)]
        vtok = [M[:, 16384 + tb * 512: 16384 + (tb + 1) * 512] for tb in range(NB)]
        ya = [M[:, 24576 + c * 2048: 24576 + (c + 1) * 2048] for c in range(2)]
        yc = [M[:, 28672 + c * 2048: 28672 + (c + 1) * 2048] for c in range(2)]
        fbuf = [M[:, c * 2048:(c + 1) * 2048] for c in range(16)]
        pT = [M[:, c * 2048:(c + 1) * 2048] for c in range(2)]
        gvb_all = M[:, 24576:28672]

        P = Prog(nc)

        rr = {"bank": 0, "ft": 0, "bt": 0}

        bank_cfg = {"lo": 0, "n": 6}

        def nbank():
            b = banks[bank_cfg["lo"] + rr["bank"] % bank_cfg["n"]]
            rr["bank"] += 1
            return b

        def ft():
            t = FT[:, rr["ft"] % 4, :]
            rr["ft"] += 1
            return t

        def bt():
            t = BT[:, rr["bt"] % 8, :]
            rr["bt"] += 1
            return t

        def tsl(tt):
            return slice(tt * 512, (tt + 1) * 512)

        def MM(out, lhsT, rhs, start, stop, **kw):
            P.add("pe", lambda e: e.matmul(out, lhsT, rhs, start=start, stop=stop, **kw),
                  reads=[lhsT, rhs], writes=[out])

        def ACT(out, in_, func, reads=None, **kw):
            rd = [in_] + (reads or [])
            P.add("act", lambda e: e.activation(out=out, in_=in_, func=func, **kw), reads=rd, writes=[out])

        def TT(eng, out, in0, in1, op):
            P.add(eng, lambda e: e.tensor_tensor(out=out, in0=in0, in1=in1, op=op), reads=[in0, in1], writes=[out])

        def STT(eng, out, in0, scalar, in1, op0, op1):
            rd = [in0, in1] + ([scalar] if not isinstance(scalar, float) else [])
            P.add(eng, lambda e: e.scalar_tensor_tensor(out=out, in0=in0, scalar=scalar, in1=in1, op0=op0, op1=op1),
                  reads=rd, writes=[out])

        def TS(eng, out, in0, scalar1, op0):
            rd = [in0] + ([scalar1] if not isinstance(scalar1, float) else [])
            P.add(eng, lambda e: e.tensor_scalar(out=out, in0=in0, scalar1=scalar1, scalar2=None, op0=op0),
                  reads=rd, writes=[out])

        def COPY(eng, out, in_):
            if eng == "act":
                P.add("act", lambda e: e.copy(out, in_), reads=[in_], writes=[out])
            else:
                P.add(eng, lambda e: e.tensor_copy(out, in_), reads=[in_], writes=[out])

        def DMA(q, out, in_, slow=False):
            kw = {"allow_slow_non_contiguous": True} if slow else {}
            return P.add(q, lambda e: e.dma_start(out=out, in_=in_, **kw), reads=[in_], writes=[out], dma=True)

        worder = []
        for l in range(L):
            for g in [3, 4, 5, 6, 7, 8, 10, 9, 1, 2, 0]:
                worder.append((("in", l, g), win_d[l, :, g * 256:(g + 1) * 256].rearrange("(k p) n -> p k n", p=128), 8, 256))
            for half in range(2):
                for g in range(4):
                    worder.append((("out", l, half, g), wout_d[l, :, g * 256:(g + 1) * 256].rearrange("(k p) n -> p k n", p=128), 8, 256))
                for e in range(8):
                    for g in range(2):
                        gg = 2 * e + g
                        worder.append((("ff1", l, half, gg), wf1_d[l, :, gg * 256:(gg + 1) * 256].rearrange("(k p) n -> p k n", p=128), 8, 256))
                    for ch in range(2):
                        worder.append((("ff2", l, half, e, ch),
                                       wf2_d[l, e * 512:(e + 1) * 512, ch * 512:(ch + 1) * 512].rearrange("(k p) n -> p k n", p=128), 4, 512))
            for g in range(4):
                worder.append((("gate", l, g), wg_d[l, :, g * 256:(g + 1) * 256].rearrange("(k p) n -> p k n", p=128), 8, 256))
                worder.append((("proj", l, g), wp_d[l, :, g * 256:(g + 1) * 256].rearrange("(k p) n -> p k n", p=128), 2, 256))
        wstate = {"next_load": 0, "next_get": 0}

        def wview(i):
            k, n = worder[i][2], worder[i][3]
            return Wt[i % NSLOT][:, 0:k * n].rearrange("p (k n) -> p k n", k=k)

        def wget(key):
            i = wstate["next_get"]
            assert worder[i][0] == key, (worder[i][0], key)
            wstate["next_get"] += 1
            while wstate["next_load"] < len(worder) and wstate["next_load"] <= i + LOOKAHEAD:
                j = wstate["next_load"]
                DMA("pool", wview(j), worder[j][1])
                wstate["next_load"] += 1
            return wview(i)

        DMA("sp", cF[:], cst_d[:, 0:3, :])
        DMA("pool", cB[:], cst_d[:, 3:7, :])

        def vcol(l, k):
            return vec[:, l * NVEC + k: l * NVEC + k + 1]

        V_G1, V_G2, V_G3, V_CW, V_GQ, V_GK, V_SG = 0, 8, 16, 24, 30, 31, 32
        def load_vecs(l, q):
            b = l * NVEC
            DMA(q, vec[:, b + V_G1: b + V_G1 + 8], n1_d[l].rearrange("(c p) -> p c", p=128), slow=True)
            DMA(q, vec[:, b + V_G2: b + V_G2 + 8], n2_d[l].rearrange("(c p) -> p c", p=128), slow=True)
            DMA(q, vec[:, b + V_G3: b + V_G3 + 8], n3_d[l].rearrange("(c p) -> p c", p=128), slow=True)
            DMA(q, vec[:, b + V_CW: b + V_CW + 6].rearrange("p (j c) -> p j c", j=3),
                cw_d[l].rearrange("j (c p) -> p j c", p=128), slow=True)
            for hh in range(2):
                DMA(q, vec[hh * 64:(hh + 1) * 64, b + V_GQ: b + V_GQ + 1], qg_d[l].rearrange("(p o) -> p o", o=1), slow=True)
                DMA(q, vec[hh * 64:(hh + 1) * 64, b + V_GK: b + V_GK + 1], kg_d[l].rearrange("(p o) -> p o", o=1), slow=True)
            DMA(q, vec[:, b + V_SG: b + V_SG + 2], sg_d[l].rearrange("(c p) -> p c", p=128), slow=True)

        def scale_vecs(l):
            b = l * NVEC
            TS("dve", vcol(l, V_GQ), vcol(l, V_GQ), 0.125, ALU.mult)
            TS("dve", vec[:, b + V_SG: b + V_SG + 2], vec[:, b + V_SG: b + V_SG + 2], 8.0, ALU.mult)


        if upto >= 0:
            load_vecs(0, "pool")

        stg = [FT[:, 0:2, :], FT[:, 2:4, :]]
        for tb in range(NB):
            st = stg[tb % 2]
            DMA("sp", st, x_d[tb * 128:(tb + 1) * 128, :].rearrange("p (a b) -> p a b", a=2))
            for half in range(2):
                bk = banks[6 + half]
                for cc in range(4):
                    c = half * 4 + cc
                    src = st[:, c // 4, (c % 4) * 128:(c % 4 + 1) * 128]
                    dst = bk[:, cc * 128:(cc + 1) * 128]
                    P.add("pe", lambda e, dst=dst, src=src: e.transpose(dst, src, ident), reads=[src, ident], writes=[dst])
                for cc in range(4):
                    c = half * 4 + cc
                    COPY("dve" if cc % 2 == 0 else "act", h[c][:, tb * 128:(tb + 1) * 128], bk[:, cc * 128:(cc + 1) * 128])

        def rmsnorm(l, vbase, tts=None, talloc=None):
            bks = {}
            tts = list(range(NT)) if tts is None else list(tts)
            talloc = ft if talloc is None else talloc

            def sq_mm(tt):
                ts_ = tsl(tt)
                for c in range(8):
                    if c in (2, 5, 7):
                        TT("pool", XY[c][:, ts_], h[c][:, ts_], h[c][:, ts_], ALU.mult)
                    else:
                        ACT(XY[c][:, ts_], h[c][:, ts_], AF.Square)
                bk = nbank()
                for c in range(8):
                    MM(bk[:], onesd, XY[c][:, ts_], c == 0, c == 7)
                bks[tt] = bk

            def fin(tt):
                ts_ = tsl(tt)
                r = talloc()
                ACT(r, bks[tt][:], AF.Ln, bias=EPS)
                ACT(r, r, AF.Exp, scale=-0.5)
                for c in range(8):
                    STT("dve", XY[c][:, ts_], h[c][:, ts_], vcol(l, vbase + c), r, ALU.mult, ALU.mult)

            sq_mm(tts[0])
            for k in range(1, len(tts)):
                sq_mm(tts[k])
                fin(tts[k - 1])
            fin(tts[-1])

        def dense_fm(wv, oc, tt, rhs_list, kcs=None):
            bk = nbank()
            n = len(rhs_list)
            for i in range(n):
                kc = i if kcs is None else kcs[i]
                MM(bk[:], wv[:, kc, oc * 128:(oc + 1) * 128], rhs_list[i][:, tsl(tt)], i == 0, i == n - 1)
            return bk

        def chk(k):
            if upto < k:
                raise _Stop()

        for l in range(L):
          try:
              chk(1)
              scale_vecs(l)
              if l + 1 < L:
                  load_vecs(l + 1, "sp")
              for g in range(4):
                  DMA("sp", bbt[(g % 2) * 64:(g % 2) * 64 + 64, g // 2, :], sb_d[l, g:g + 1, :].broadcast_to([64, 128]))
              swst = ft()
              DMA("sp", swst.rearrange("p (g s) -> p g s", g=4), sw_d[l].rearrange("g t s -> t g s"))
              bk = banks[6]
              for g in range(4):
                  src = swst[:, g * 128:(g + 1) * 128]
                  dst = bk[:, g * 128:(g + 1) * 128]
                  P.add("pe", lambda e, dst=dst, src=src: e.transpose(dst, src, ident), reads=[src, ident], writes=[dst])
              for g in range(4):
                  TT("dve", wmT[:, g, :], bk[:, g * 128:(g + 1) * 128], mask_incl, ALU.mult)

              rmsnorm(l, V_G1)

              chk(2)
              qk_pend = [None]
              for which, dstT, gcol in (("q", qT, V_GQ), ("k", kT, V_GK)):
                  for gi in range(2):
                      wv = wget(("in", l, (3 if which == "q" else 5) + gi))
                      for oc in range(2):
                          qc = gi * 2 + oc
                          for tt in range(NT):
                              bk = dense_fm(wv, oc, tt, XY)
                              sq = bt()
                              ACT(sq, bk[:], AF.Square)
                              if qk_pend[0] is not None:
                                  qk_pend[0]()

                              def fin(bk=bk, sq=sq, dst=dstT[qc][:, tsl(tt)], gcol=gcol):
                                  b2 = nbank()
                                  MM(b2[:], blockones, sq, True, True)
                                  r = ft()
                                  ACT(r, b2[:], AF.Ln, bias=EPS)
                                  ACT(r, r, AF.Exp, scale=-0.5)
                                  STT("dve", dst, bk[:], vcol(l, gcol), r, ALU.mult, ALU.mult)
                              qk_pend[0] = fin
              if qk_pend[0] is not None:
                  qk_pend[0]()
                  qk_pend[0] = None
              chk(3)
              for gi in range(2):
                  wv = wget(("in", l, 7 + gi))
                  for tb in range(NB):
                      bk = nbank()
                      for kc in range(8):
                          MM(bk[:, 0:256], XY[kc][:, tb * 128:(tb + 1) * 128], wv[:, kc, :], kc == 0, kc == 7)
                      COPY("dve" if tb % 2 == 0 else "act", vtok[tb][:, gi * 256:(gi + 1) * 256], bk[:, 0:256])
              wv = wget(("in", l, 10))
              for tb in range(NB):
                  bk = nbank()
                  for kc in range(8):
                      MM(bk[:, 0:256], XY[kc][:, tb * 128:(tb + 1) * 128], wv[:, kc, :], kc == 0, kc == 7)
                  ACT(gvb_all[:, tb * 256:(tb + 1) * 256], bk[:, 0:256], AF.Gelu)
              sqall = BT[:].rearrange("p a b -> p (a b)")
              TT("dve", sqall, gvb_all, gvb_all, ALU.mult)
              P.add("dve", lambda e: e.tensor_reduce(out=sst[:], in_=sqall.rearrange("p (g e) -> p g e", e=64), axis=AX.X, op=ALU.add),
                    reads=[sqall], writes=[sst[:]])
              ACT(sst[:], sst[:], AF.Ln, bias=64.0 * EPS)
              ACT(sst[:], sst[:], AF.Exp, scale=-0.5)
              gv3 = gvb_all.rearrange("p (g e) -> p g e", e=64)
              TT("dve", gv3, gv3, sst[:].unsqueeze(2).broadcast_to([128, 64, 64]), ALU.mult)
              wv = wget(("in", l, 9))
              for oc in range(2):
                  for tt in range(NT):
                      bk = dense_fm(wv, oc, tt, XY)
                      ACT(yc[oc][:, tsl(tt)], bk[:], AF.Gelu)
              for tb2 in range(NB // 2):
                  bk = nbank()
                  for t2 in range(2):
                      tb = tb2 * 2 + t2
                      for g in range(4):
                          gp, gh = g // 2, g % 2
                          o = bk[gh * 64:(gh + 1) * 64, (t2 * 2 + gp) * 128:(t2 * 2 + gp + 1) * 128]
                          lhs = gvb_all[:, tb * 256 + g * 64: tb * 256 + (g + 1) * 64]
                          kw = {"tile_position": (0, 64)} if gh == 1 else {}
                          MM(o, lhs, wmT[:, g, :], True, True, **kw)
                  for t2 in range(2):
                      tb = tb2 * 2 + t2
                      for gp in range(2):
                          tmp = ft()[:, 0:128]
                          STT("dve", tmp, bk[:, (t2 * 2 + gp) * 128:(t2 * 2 + gp + 1) * 128],
                              vcol(l, V_SG + gp), bbt[:, gp, :], ALU.mult, ALU.add)
                          TT("dve", yc[gp][:, tb * 128:(tb + 1) * 128], tmp, yc[gp][:, tb * 128:(tb + 1) * 128], ALU.mult)
              chk(4)
              wc = wget(("in", l, 1))
              wh = wget(("in", l, 2))
              wb = wget(("in", l, 0))
              for oc in range(2):
                  for tt in range(NT):
                      gcur = G2[:, tt % 2, :]
                      gprev = G2[:, (tt + 1) % 2, :]
                      bk1 = dense_fm(wc, oc, tt, XY)
                      tc_ = ft()
                      COPY("act", tc_, bk1[:])
                      bk2 = dense_fm(wh, oc, tt, XY)
                      TT("dve", gcur[:, 2:514], tc_, bk2[:], ALU.mult)
                      if tt == 0:
                          P.add("dve", lambda e, gcur=gcur: e.memset(gcur[:, 0:2], 0.0), writes=[gcur[:, 0:2]])
                      else:
                          COPY("dve", gcur[:, 0:2], gprev[:, 512:514])
                      o = ft()
                      TS("dve", o, gcur[:, 2:514], vcol(l, V_CW + 2 * 2 + oc), ALU.mult)
                      STT("dve", o, gcur[:, 1:513], vcol(l, V_CW + 1 * 2 + oc), o, ALU.mult, ALU.add)
                      STT("dve", o, gcur[:, 0:512], vcol(l, V_CW + 0 * 2 + oc), o, ALU.mult, ALU.add)
                      bk3 = dense_fm(wb, oc, tt, XY)
                      TT("dve", ya[oc][:, tsl(tt)], bk3[:], o, ALU.mult)

              chk(5)

              def run_attention(steps, pbank, obank_of, per_iter=None):
                  nst = len(steps)

                  def geom(i):
                      hp, hh, tt, kb, first, last, sp_ = steps[i]
                      j = kb - 4 * tt
                      c0 = j * 128 if j >= 0 else 0
                      return hp, hh, tt, kb, first, last, sp_, j, c0, slice(hh * 64, (hh + 1) * 64)

                  def pe_z(i):
                      hp, hh, tt, kb, first, last, sp_, j, c0, rows = geom(i)
                      zb = banks[i % 2]
                      MM(zb[:, c0:512], kT[hp][rows, kb * 128:(kb + 1) * 128], qT[hp][rows, tt * 512 + c0:(tt + 1) * 512], True, True)

                  def act_1(i):
                      hp, hh, tt, kb, first, last, sp_, j, c0, rows = geom(i)
                      zb = banks[i % 2]
                      E = FT[:, i % 4, :]
                      Lp = BT[:, i % 4, :]
                      ACT(E[:, c0:512], zb[:, c0:512], AF.Exp)
                      if j >= 0:
                          TT("dve", E[:, c0:c0 + 128], E[:, c0:c0 + 128], mask_strict, ALU.mult)
                          if c0 + 128 < 512:
                              ACT(Lp[:, c0 + 128:512], E[:, c0 + 128:512], AF.Ln, bias=1.0)
                          ACT(Lp[:, c0:c0 + 128], E[:, c0:c0 + 128], AF.Ln, bias=1.0)
                      else:
                          ACT(Lp[:, c0:512], E[:, c0:512], AF.Ln, bias=1.0)

                  def pe_tri(i):
                      hp, hh, tt, kb, first, last, sp_, j, c0, rows = geom(i)
                      pb = pbank(sp_)
                      Lp = BT[:, i % 4, :]
                      MM(pb[:, c0:512], negTriUI, Lp[:, c0:512], first, True, skip_group_check=True)

                  def act_2(i):
                      hp, hh, tt, kb, first, last, sp_, j, c0, rows = geom(i)
                      pb = pbank(sp_)
                      E = FT[:, i % 4, :]
                      X = BT[:, 4 + i % 2, :]
                      A = BT[:, 6 + i % 2, :]
                      ACT(X[:, c0:512], pb[:, c0:512], AF.Exp)
                      TT("dve", A[:, c0:512], E[:, c0:512], X[:, c0:512], ALU.mult)

                  def pe_3(i):
                      hp, hh, tt, kb, first, last, sp_, j, c0, rows = geom(i)
                      pb = pbank(sp_)
                      obank = obank_of(sp_)
                      Lp = BT[:, i % 4, :]
                      A = BT[:, 6 + i % 2, :]
                      kw = {"tile_position": (0, 64)} if hh == 1 else {}
                      MM(obank[rows, c0:512], vtok[kb][:, (2 * hp + hh) * 64:(2 * hp + hh + 1) * 64], A[:, c0:512],
                         first, last, skip_group_check=True, **kw)
                      if not last:
                          MM(pb[:, c0:512], negTriL, Lp[:, c0:512], False, True, skip_group_check=True)
                      else:
                          COPY("dve", qT[hp][rows, tsl(tt)], obank[rows, :])

                  for t in range(nst + 5):
                      if 0 <= t - 4 < nst:
                          pe_3(t - 4)
                      if 0 <= t - 2 < nst:
                          pe_tri(t - 2)
                      if 0 <= t < nst:
                          pe_z(t)
                      if 0 <= t - 3 < nst:
                          act_2(t - 3)
                      if 0 <= t - 1 < nst:
                          act_1(t - 1)
                      if per_iter is not None:
                          per_iter(t)

              def dense_half(half):
                  tts = [0, 1] if half == 0 else [2, 3]
                  oth = slice(1024, 2048) if half == 0 else slice(0, 1024)
                  fX = [XY[c][:, oth] for c in range(4)]
                  ntmp = 2 if half == 0 else 4
                  tmps = [XY[4 + k][:, oth].bitcast(F32) for k in range(ntmp)]
                  tstate = [0]

                  def talloc():
                      t = tmps[tstate[0] % ntmp]
                      tstate[0] += 1
                      return t

                  def ltsl(tt):
                      lt = tt - tts[0]
                      return slice(lt * 512, (lt + 1) * 512)

                  ysrc = [ya[0], ya[1], qT[0], qT[1], qT[2], qT[3], yc[0], yc[1]]
                  for g in range(4):
                      wv = wget(("out", l, half, g))
                      for oc in range(2):
                          o = g * 2 + oc
                          for tt in tts:
                              bk = dense_fm(wv, oc, tt, ysrc)
                              TT("dve", h[o][:, tsl(tt)], h[o][:, tsl(tt)], bk[:], ALU.add)
                              yield
                  rmsnorm(l, V_G2, tts, talloc)
                  yield
                  for e in range(8):
                      for g in range(2):
                          wv = wget(("ff1", l, half, 2 * e + g))
                          for oc in range(2):
                              fc = g * 2 + oc
                              for tt in tts:
                                  bk = dense_fm(wv, oc, tt, XY)
                                  t1 = talloc()
                                  ACT(t1, bk[:], AF.Relu)
                                  TT("dve", fX[fc][:, ltsl(tt)], t1, t1, ALU.mult)
                                  yield
                      for ch in range(2):
                          wv = wget(("ff2", l, half, e, ch))
                          for oc4 in range(4):
                              o = ch * 4 + oc4
                              for tt in tts:
                                  bk = nbank()
                                  for kc in range(4):
                                      MM(bk[:], wv[:, kc, oc4 * 128:(oc4 + 1) * 128], fX[kc][:, ltsl(tt)], kc == 0, kc == 3)
                                  TT("dve", h[o][:, tsl(tt)], h[o][:, tsl(tt)], bk[:], ALU.add)
                                  yield

              stepsA = []
              for pg in range(2):
                  for tt in (0, 1):
                      kmax = 4 * tt + 3
                      for kb in range(kmax, -1, -1):
                          for sp_ in range(4):
                              stepsA.append((2 * pg + sp_ // 2, sp_ % 2, tt, kb, kb == kmax, kb == 0, sp_))
              run_attention(stepsA, lambda sp_: banks[2 + sp_], lambda sp_: banks[6 + sp_ // 2])

              chk(6)
              stepsB = []
              for hp in range(4):
                  for tt in (2, 3):
                      kmax = 4 * tt + 3
                      for kb in range(kmax, -1, -1):
                          for hh in range(2):
                              stepsB.append((hp, hh, tt, kb, kb == kmax, kb == 0, hh))
              genA = dense_half(0)
              bank_cfg["lo"], bank_cfg["n"] = 5, 3

              def pull(t):
                  next(genA, None)
                  next(genA, None)

              def run_attention_pairs(pairs, per_iter):
                  npair = len(pairs)
                  Abuf = [XY[6 + k // 2][:, 1024 + (k % 2) * 512: 1024 + (k % 2 + 1) * 512] for k in range(4)]

                  def geom(u, hh):
                      hp, tt, kb, first, last = pairs[u]
                      j = kb - 4 * tt
                      c0 = j * 128 if j >= 0 else 0
                      return hp, tt, kb, first, last, j, c0, slice(hh * 64, (hh + 1) * 64)

                  def pe_z(u):
                      for hh in range(2):
                          hp, tt, kb, first, last, j, c0, rows = geom(u, hh)
                          zb = banks[hh]
                          MM(zb[:, c0:512], kT[hp][rows, kb * 128:(kb + 1) * 128], qT[hp][rows, tt * 512 + c0:(tt + 1) * 512], True, True)

                  def act_1(u):
                      for hh in range(2):
                          hp, tt, kb, first, last, j, c0, rows = geom(u, hh)
                          zb = banks[hh]
                          E = FT[:, (2 * u + hh) % 4, :]
                          Lp = BT[:, (2 * u + hh) % 6, :]
                          ACT(E[:, c0:512], zb[:, c0:512], AF.Exp)
                          if j >= 0:
                              TT("dve", E[:, c0:c0 + 128], E[:, c0:c0 + 128], mask_strict, ALU.mult)
                              if c0 + 128 < 512:
                                  ACT(Lp[:, c0 + 128:512], E[:, c0 + 128:512], AF.Ln, bias=1.0)
                              ACT(Lp[:, c0:c0 + 128], E[:, c0:c0 + 128], AF.Ln, bias=1.0)
                          else:
                              ACT(Lp[:, c0:512], E[:, c0:512], AF.Ln, bias=1.0)

                  def pe_tri(u):
                      for hh in range(2):
                          hp, tt, kb, first, last, j, c0, rows = geom(u, hh)
                          Lp = BT[:, (2 * u + hh) % 6, :]
                          MM(banks[2 + hh][:, c0:512], negTriUI, Lp[:, c0:512], first, True, skip_group_check=True)

                  def act_2(u):
                      for hh in range(2):
                          hp, tt, kb, first, last, j, c0, rows = geom(u, hh)
                          E = FT[:, (2 * u + hh) % 4, :]
                          X = BT[:, 6 + hh, :]
                          A = Abuf[(2 * u + hh) % 4]
                          ACT(X[:, c0:512], banks[2 + hh][:, c0:512], AF.Exp)
                          TT("dve", A[:, c0:512], E[:, c0:512], X[:, c0:512], ALU.mult)

                  def pe_3(u):
                      obank = banks[4]
                      for hh in range(2):
                          hp, tt, kb, first, last, j, c0, rows = geom(u, hh)
                          A = Abuf[(2 * u + hh) % 4]
                          kw = {"tile_position": (0, 64)} if hh == 1 else {}
                          MM(obank[rows, c0:512], vtok[kb][:, (2 * hp + hh) * 64:(2 * hp + hh + 1) * 64], A[:, c0:512],
                             first, last, skip_group_check=True, **kw)
                      for hh in range(2):
                          hp, tt, kb, first, last, j, c0, rows = geom(u, hh)
                          Lp = BT[:, (2 * u + hh) % 6, :]
                          if not last:
                              MM(banks[2 + hh][:, c0:512], negTriL, Lp[:, c0:512], False, True, skip_group_check=True)
                          else:
                              COPY("dve", qT[hp][rows, tsl(tt)], obank[rows, :])

                  for u in range(npair + 2):
                      if 0 <= u - 2 < npair:
                          pe_3(u - 2)
                      if 0 <= u - 1 < npair:
                          pe_tri(u - 1)
                      if 0 <= u < npair:
                          pe_z(u)
                      if 0 <= u - 1 < npair:
                          act_2(u - 1)
                      if 0 <= u < npair:
                          act_1(u)
                      per_iter(u)

              pairsB = []
              for hp in range(4):
                  for tt in (2, 3):
                      kmax = 4 * tt + 3
                      for kb in range(kmax, -1, -1):
                          pairsB.append((hp, tt, kb, kb == kmax, kb == 0))
              run_attention_pairs(pairsB, pull)
              for _ in genA:
                  pass
              bank_cfg["lo"], bank_cfg["n"] = 0, 6
              for _ in dense_half(1):
                  pass

              chk(8)
              for q4 in range(4):
                  st_ = stg[q4 % 2]
                  DMA("sp", st_.rearrange("p a b -> p (a b)").rearrange("p (t f) -> p t f", t=4),
                      p_d[l, q4 * 512:(q4 + 1) * 512, :].rearrange("(t p) f -> p t f", p=128))
                  for fc in range(2):
                      bk = banks[6 + fc]
                      for t4 in range(4):
                          e0 = t4 * 256 + fc * 128
                          src = st_[:, e0 // 512, (e0 % 512):(e0 % 512) + 128]
                          dst = bk[:, t4 * 128:(t4 + 1) * 128]
                          P.add("pe", lambda e, dst=dst, src=src: e.transpose(dst, src, ident), reads=[src, ident], writes=[dst])
                      COPY("dve" if fc == 0 else "act", pT[fc][:, tsl(q4)], bk[:])
              rmsnorm(l, V_G3)
              for g in range(4):
                  wv = wget(("gate", l, g))
                  wpv = wget(("proj", l, g))
                  for oc in range(2):
                      o = g * 2 + oc
                      for tt in range(NT):
                          bk = dense_fm(wv, oc, tt, XY)
                          gt = ft()
                          ACT(gt, bk[:], AF.Sigmoid)
                          bk2 = nbank()
                          for fc in range(2):
                              MM(bk2[:], wpv[:, fc, oc * 128:(oc + 1) * 128], pT[fc][:, tsl(tt)], fc == 0, fc == 1)
                          TT("dve", gt, gt, bk2[:], ALU.mult)
                          TT("dve", h[o][:, tsl(tt)], h[o][:, tsl(tt)], gt, ALU.add)

          except _Stop:
              break

        for tb in range(NB):
            st = stg[tb % 2]
            for half in range(2):
                bk = banks[6 + half]
                for cc in range(4):
                    c = half * 4 + cc
                    src = h[c][:, tb * 128:(tb + 1) * 128]
                    dst = bk[:, cc * 128:(cc + 1) * 128]
                    P.add("pe", lambda e, dst=dst, src=src: e.transpose(dst, src, ident), reads=[src, ident], writes=[dst])
                COPY("dve" if half == 0 else "act", st[:, half, :], bk[:])
            op = DMA("sp", out_d[tb * 128:(tb + 1) * 128, :].rearrange("p (a b) -> p a b", a=2), st)
            P.mark_final(op)

        assert upto < 99 or wstate["next_get"] == len(worder), (wstate, len(worder))
        stats = P.emit()
    return nc, stats


WNAMES = ["norm1_g", "w_in", "conv_w", "q_norm_g", "k_norm_g", "sgu_norm_g", "sgu_w", "sgu_b", "w_out",
          "norm2_g", "w_ff1", "w_ff2", "norm3_g", "w_ple_gate", "w_ple_proj"]

_CACHE = {}


def _get_prog(n_layers):
    if n_layers not in _CACHE:
        _CACHE[n_layers] = build(n_layers)[0]
    return _CACHE[n_layers]


def kernel(**inputs):
    x = np.ascontiguousarray(np.asarray(inputs["x"], dtype=np.float32))
    p = np.asarray(inputs["p"], dtype=np.float32)
    depth = p.shape[0]
    nb = x.shape[0]
    cst = make_consts()
    nc = _get_prog(depth)
    wts = {k: np.ascontiguousarray(np.asarray(inputs[k], dtype=np.float32)) for k in WNAMES}
    in_maps = []
    for b in range(nb):
        m = {"x": x[b], "p": np.ascontiguousarray(p[:, b]), "cst": cst}
        m.update(wts)
        in_maps.append(m)
    res = run_bass_kernel_spmd(nc, in_maps, core_ids=list(range(nb)))
    out = np.stack([np.asarray(r["out"], dtype=np.float32) for r in res.results], axis=0)
    return out
```
